# Optimizing a Trainium2 kernel written in Bass

```python
import math
import jax, jax.numpy as jnp
from jax import lax
import numpy as np

D_MODEL = 1024
BATCH = 16
SEQ = 2048
DEPTH = 1

CTX_LEN = 256
GRID_W = 64

MIX_WIDTH = D_MODEL
DN_HEAD_DIM = 128
DN_WIDTH = MIX_WIDTH // 2
DN_HEADS = DN_WIDTH // DN_HEAD_DIM
NA_HEAD_DIM = 64
NA_WIDTH = MIX_WIDTH - DN_WIDTH
NA_HEADS = NA_WIDTH // NA_HEAD_DIM

DN_IN = 4 * DN_WIDTH + 4 * DN_HEADS
NA_IN = 3 * NA_WIDTH
IN_COLS = DN_IN + NA_IN

CONV_K = 5
CHUNK = 64

WIN_ROWS_MAX = 8
WIN_COLS = 16
QB_W = 16
KB_W = 32
N_COL_BLK = GRID_W // QB_W

D_FF = -(-8 * D_MODEL // (3 * 256)) * 256
EPS = 1e-6

kernel_name = "hymba_gdn_natten_dit_prefix"


def rms_norm(x, w):
    xf = x.astype(jnp.float32)
    y = xf * lax.rsqrt(jnp.mean(xf * xf, axis=-1, keepdims=True) + EPS)
    return (y * w.astype(jnp.float32)).astype(x.dtype)


def l2_normalize(x):
    xf = x.astype(jnp.float32)
    return xf * lax.rsqrt(jnp.sum(xf * xf, axis=-1, keepdims=True) + EPS)


def modulate(h, shift, scale):
    return h * (1 + scale) + shift


def short_conv(u, w):
    out = lax.conv_general_dilated(
        u, w[:, None, :].astype(u.dtype), window_strides=(1,),
        padding=[(CONV_K // 2, CONV_K // 2)],
        dimension_numbers=("NWC", "WIO", "NWC"), feature_group_count=u.shape[-1])
    return jax.nn.silu(out)


def chunk_gated_delta_rule(q, k, v, g, beta, S0):
    B, H, L, dk = q.shape
    dv = v.shape[-1]
    n = L // CHUNK
    f32 = jnp.float32
    q = q.astype(f32).reshape(B, H, n, CHUNK, dk)
    k = k.astype(f32).reshape(B, H, n, CHUNK, dk)
    v = v.astype(f32).reshape(B, H, n, CHUNK, dv)
    beta = beta.astype(f32).reshape(B, H, n, CHUNK)
    gam = jnp.cumsum(g.astype(f32).reshape(B, H, n, CHUNK), axis=-1)
    tril = jnp.tril(jnp.ones((CHUNK, CHUNK), dtype=bool))
    strict = jnp.tril(jnp.ones((CHUNK, CHUNK), dtype=bool), -1)
    decay = jnp.exp(jnp.where(tril, gam[..., :, None] - gam[..., None, :], -jnp.inf))
    kb = k * beta[..., None]
    m = jnp.where(strict, jnp.einsum("bhncd,bhnmd->bhncm", kb, k) * decay, 0.0)
    a_mat = jnp.eye(CHUNK, dtype=f32) + m
    u = lax.linalg.triangular_solve(a_mat, v * beta[..., None], left_side=True, lower=True, unit_diagonal=True)
    w = lax.linalg.triangular_solve(a_mat, kb * jnp.exp(gam)[..., None], left_side=True, lower=True, unit_diagonal=True)
    qk = jnp.where(tril, jnp.einsum("bhncd,bhnmd->bhncm", q, k) * decay, 0.0)

    def step(S, xs):
        q_i, k_i, u_i, w_i, qk_i, gam_i = xs
        v_new = u_i - jnp.einsum("bhck,bhkv->bhcv", w_i, S)
        o_i = (jnp.einsum("bhck,bhkv->bhcv", q_i * jnp.exp(gam_i)[..., None], S)
               + jnp.einsum("bhcm,bhmv->bhcv", qk_i, v_new))
        g_last = gam_i[..., -1:]
        k_dec = k_i * jnp.exp(g_last - gam_i)[..., None]
        S = S * jnp.exp(g_last)[..., None] + jnp.einsum("bhck,bhcv->bhkv", k_dec, v_new)
        return S, o_i

    xs = tuple(jnp.moveaxis(t, 2, 0) for t in (q, k, u, w, qk, gam))
    S, o = lax.scan(step, S0.astype(f32), xs)
    o = jnp.moveaxis(o, 0, 2).reshape(B, H, L, dv)
    return o, S


def gated_deltanet(p, conv_w, A_log, dt_bias, out_norm_w, S0_f, S0_b):
    B, L, _ = p.shape
    qkv = short_conv(p[..., :3 * DN_WIDTH], conv_w)
    z = p[..., 3 * DN_WIDTH:4 * DN_WIDTH].reshape(B, L, DN_HEADS, DN_HEAD_DIM)
    a = p[..., 4 * DN_WIDTH:4 * DN_WIDTH + 2 * DN_HEADS].reshape(B, L, 2, DN_HEADS)
    b = p[..., 4 * DN_WIDTH + 2 * DN_HEADS:].reshape(B, L, 2, DN_HEADS)
    q, k, v = [t.reshape(B, L, DN_HEADS, DN_HEAD_DIM).transpose(0, 2, 1, 3) for t in jnp.split(qkv, 3, axis=-1)]
    q = l2_normalize(q) * DN_HEAD_DIM ** -0.5
    k = l2_normalize(k)
    g = (-jnp.exp(A_log.astype(jnp.float32))
         * jax.nn.softplus(a.astype(jnp.float32) + dt_bias.astype(jnp.float32))).transpose(2, 0, 3, 1)
    beta = jax.nn.sigmoid(b.astype(jnp.float32)).transpose(2, 0, 3, 1)
    o_f, S_f = chunk_gated_delta_rule(q, k, v, g[0], beta[0], S0_f)
    rev = lambda t: jnp.flip(t, axis=2)
    o_b, S_b = chunk_gated_delta_rule(rev(q), rev(k), rev(v), rev(g[1]), rev(beta[1]), S0_b)
    o = (o_f + rev(o_b)).transpose(0, 2, 1, 3)
    o = rms_norm(o, out_norm_w) * jax.nn.silu(z.astype(jnp.float32))
    return o.reshape(B, L, DN_WIDTH).astype(p.dtype), S_f, S_b


def na_qkv(p, q_norm_w, k_norm_w):
    B, L, _ = p.shape
    q, k, v = [t.reshape(B, L, NA_HEADS, NA_HEAD_DIM) for t in jnp.split(p, 3, axis=-1)]
    q = rms_norm(q, q_norm_w) * NA_HEAD_DIM ** -0.5
    k = rms_norm(k, k_norm_w)
    return q, k, v


def na_column_tables():
    cols = np.arange(GRID_W)
    win_start = np.clip(cols - WIN_COLS // 2, 0, GRID_W - WIN_COLS).reshape(N_COL_BLK, QB_W)
    blk_start = np.clip(np.arange(N_COL_BLK) * QB_W - WIN_COLS // 2, 0, GRID_W - KB_W)
    key_cols = blk_start[:, None] + np.arange(KB_W)
    q_cols = cols.reshape(N_COL_BLK, QB_W)
    kc = key_cols[:, None, :]
    valid = (kc >= win_start[:, :, None]) & (kc < win_start[:, :, None] + WIN_COLS)
    rel_idx = np.clip(kc - q_cols[:, :, None] + WIN_COLS - 1, 0, 2 * WIN_COLS - 2)
    return key_cols, valid, rel_idx


def neighborhood_attention(q, k, v, k_ctx, v_ctx, rpb, rows):
    B, L, H, Dh = q.shape
    win_rows = min(WIN_ROWS_MAX, rows)
    n_win = win_rows * KB_W
    key_cols, col_valid, rel_col_idx = na_column_tables()
    qg = q.reshape(B, rows, N_COL_BLK, QB_W, H, Dh)
    kg = k.reshape(B, rows, GRID_W, H, Dh)
    vg = v.reshape(B, rows, GRID_W, H, Dh)
    bias_cols = rpb[:, :, rel_col_idx]

    def row_block(r):
        r0 = jnp.clip(r - win_rows // 2, 0, rows - win_rows)
        q_r = lax.dynamic_index_in_dim(qg, r, axis=1, keepdims=False)
        k_r = lax.dynamic_slice_in_dim(kg, r0, win_rows, axis=1)[:, :, key_cols]
        v_r = lax.dynamic_slice_in_dim(vg, r0, win_rows, axis=1)[:, :, key_cols]
        dr_idx = r0 + jnp.arange(win_rows) - r + WIN_ROWS_MAX - 1
        bias = jnp.take(bias_cols, dr_idx, axis=1).transpose(0, 2, 3, 1, 4)
        s_win = jnp.einsum("bnqhd,binchd->bhnqic", q_r, k_r).astype(jnp.float32) + bias.astype(jnp.float32)
        s_win = jnp.where(col_valid[:, :, None, :], s_win, -jnp.inf)
        s_ctx = jnp.einsum("bnqhd,bkhd->bhnqk", q_r, k_ctx).astype(jnp.float32)
        s = jnp.concatenate([s_win.reshape(B, H, N_COL_BLK, QB_W, n_win), s_ctx], axis=-1)
        p = jax.nn.softmax(s, axis=-1).astype(v.dtype)
        p_win = p[..., :n_win].reshape(B, H, N_COL_BLK, QB_W, win_rows, KB_W)
        o = (jnp.einsum("bhnqic,binchd->bnqhd", p_win, v_r)
             + jnp.einsum("bhnqk,bkhd->bnqhd", p[..., n_win:], v_ctx))
        return o.reshape(B, GRID_W, H, Dh)

    out = lax.map(row_block, jnp.arange(rows))
    return jnp.moveaxis(out, 0, 1).reshape(B, L, H * Dh)


def context_attention(q, k, v):
    B, N, H, Dh = q.shape
    s = jnp.einsum("bqhd,bkhd->bhqk", q, k).astype(jnp.float32)
    p = jax.nn.softmax(s, axis=-1).astype(v.dtype)
    return jnp.einsum("bhqk,bkhd->bqhd", p, v).reshape(B, N, H * Dh)


def swiglu(h, w_in, w_out):
    gate, up = jnp.split(h @ w_in, 2, axis=-1)
    return (jax.nn.silu(gate) * up) @ w_out


def setup_inputs(seed: int = 0) -> dict:
    key = jax.random.key(seed)
    ks = jax.random.split(key, 20)
    nrm = lambda k, shape, s: jax.random.normal(k, shape, jnp.float32) * s
    A = jax.random.uniform(ks[10], (DEPTH, 2, DN_HEADS), jnp.float32, minval=1.0, maxval=16.0)
    dt = jnp.exp(jax.random.uniform(ks[11], (DEPTH, 2, DN_HEADS), jnp.float32,
                                    minval=math.log(1e-3), maxval=math.log(1e-1)))
    return {
        "x": nrm(ks[0], (BATCH, SEQ, D_MODEL), 1.0),
        "c": nrm(ks[1], (BATCH, D_MODEL), 1.0),
        "ctx": nrm(ks[2], (BATCH, CTX_LEN, D_MODEL), 1.0),
        "c_ctx": nrm(ks[3], (D_MODEL,), 1.0),
        "norm1_w": 1.0 + nrm(ks[4], (DEPTH, D_MODEL), 0.02),
        "norm2_w": 1.0 + nrm(ks[5], (DEPTH, D_MODEL), 0.02),
        "w_ada": nrm(ks[6], (DEPTH, D_MODEL, 6 * D_MODEL), 0.5 * D_MODEL ** -0.5),
        "b_ada": nrm(ks[7], (DEPTH, 6 * D_MODEL), 0.02),
        "w_in": nrm(ks[8], (DEPTH, D_MODEL, IN_COLS), D_MODEL ** -0.5),
        "dn_conv_w": nrm(ks[9], (DEPTH, CONV_K, 3 * DN_WIDTH), CONV_K ** -0.5),
        "dn_A_log": jnp.log(A),
        "dn_dt_bias": dt + jnp.log(-jnp.expm1(-dt)),
        "dn_out_norm_w": 1.0 + nrm(ks[12], (DEPTH, DN_HEAD_DIM), 0.02),
        "na_q_norm_w": 1.0 + nrm(ks[13], (DEPTH, NA_HEAD_DIM), 0.02),
        "na_k_norm_w": 1.0 + nrm(ks[14], (DEPTH, NA_HEAD_DIM), 0.02),
        "na_rpb": nrm(ks[15], (DEPTH, NA_HEADS, 2 * WIN_ROWS_MAX - 1, 2 * WIN_COLS - 1), 0.1),
        "w_out": nrm(ks[16], (DEPTH, MIX_WIDTH, D_MODEL), MIX_WIDTH ** -0.5),
        "w_ffn_in": nrm(ks[17], (DEPTH, D_MODEL, 2 * D_FF), D_MODEL ** -0.5),
        "w_ffn_out": nrm(ks[18], (DEPTH, D_FF, D_MODEL), D_FF ** -0.5),
    }


def reference(x, c, ctx, c_ctx, norm1_w, norm2_w, w_ada, b_ada, w_in, dn_conv_w, dn_A_log, dn_dt_bias,
              dn_out_norm_w, na_q_norm_w, na_k_norm_w, na_rpb, w_out, w_ffn_in, w_ffn_out):
    B, L, _ = x.shape
    rows = L // GRID_W
    for i in range(DEPTH):
        last = i == DEPTH - 1
        mod_x = (jax.nn.silu(c) @ w_ada[i] + b_ada[i])[:, None, :]
        mod_c = jax.nn.silu(c_ctx) @ w_ada[i] + b_ada[i]
        sh1, sc1, g1, sh2, sc2, g2 = jnp.split(mod_x, 6, axis=-1)
        csh1, csc1, cg1, csh2, csc2, cg2 = jnp.split(mod_c, 6, axis=-1)

        px = modulate(rms_norm(x, norm1_w[i]), sh1, sc1) @ w_in[i]
        pc = modulate(rms_norm(ctx, norm1_w[i]), csh1, csc1) @ w_in[i]

        s_zero = jnp.zeros((B, DN_HEADS, DN_HEAD_DIM, DN_HEAD_DIM), jnp.float32)
        dn_c, s_ctx_f, s_ctx_b = gated_deltanet(pc[..., :DN_IN], dn_conv_w[i], dn_A_log[i], dn_dt_bias[i],
                                                dn_out_norm_w[i], s_zero, s_zero)
        dn_x, _, _ = gated_deltanet(px[..., :DN_IN], dn_conv_w[i], dn_A_log[i], dn_dt_bias[i],
                                    dn_out_norm_w[i], s_ctx_f, s_ctx_b)

        qx, kx, vx = na_qkv(px[..., DN_IN:], na_q_norm_w[i], na_k_norm_w[i])
        qc, kc, vc = na_qkv(pc[..., DN_IN:], na_q_norm_w[i], na_k_norm_w[i])
        na_x = neighborhood_attention(qx, kx, vx, kc, vc, na_rpb[i], rows)

        x = x + g1 * (jnp.concatenate([dn_x, na_x], axis=-1) @ w_out[i])
        x = x + g2 * swiglu(modulate(rms_norm(x, norm2_w[i]), sh2, sc2), w_ffn_in[i], w_ffn_out[i])

        if not last:
            na_c = context_attention(qc, kc, vc)
            ctx = ctx + cg1 * (jnp.concatenate([dn_c, na_c], axis=-1) @ w_out[i])
            ctx = ctx + cg2 * swiglu(modulate(rms_norm(ctx, norm2_w[i]), csh2, csc2), w_ffn_in[i], w_ffn_out[i])
    return x
```

```python
import numpy as np
from contextlib import ExitStack
import ml_dtypes
import concourse.bass as bass
import concourse.mybir as mybir
from concourse.bass_utils import run_bass_kernel_spmd

F32 = mybir.dt.float32
BF16 = mybir.dt.bfloat16
AF = mybir.ActivationFunctionType
ALU = mybir.AluOpType
AX = mybir.AxisListType

NCORES = 8
D = 1024
SEQ = 2048
CTX = 256
NT = SEQ + CTX
NTT = NT // 128
INC = 3600
DFF = 2816
EPS = 1e-6
NEG = -30000.0
NCONST = 13
DN_STEPS = 18
HH_ORDER = (0, 1, 2, 3)
DN_STAGE = 3


class Tl:
    __slots__ = ("name", "writers", "readers", "war", "base", "excl")

    def __init__(self, name):
        self.name = name
        self.writers = {}
        self.readers = {}
        self.war = {}
        self.base = {}
        self.excl = None


class Buf:
    def __init__(self, t, name):
        self.t = t
        self.name = name
        self.tl = Tl(name)
        self._sub = {}

    def sub(self, key):
        if key not in self._sub:
            self._sub[key] = Tl("%s[%s]" % (self.name, key))
            self._sub[key].excl = self.tl.excl
        return self._sub[key]

    def __getitem__(self, idx):
        return self.t[idx]


class Sched:
    CE = ("pe", "act", "dve", "pool")

    def __init__(self, nc, es):
        self.nc = nc
        self.sem = {e: es.enter_context(nc.semaphore("s_" + e)) for e in self.CE}
        self.cnt = {e: 0 for e in self.CE}
        self.ring = {
            "sp": [es.enter_context(nc.semaphore("d_sp%d" % i)) for i in range(24)],
            "pool": [es.enter_context(nc.semaphore("d_pl%d" % i)) for i in range(8)],
            "act": [es.enter_context(nc.semaphore("d_ac%d" % i)) for i in range(8)],
        }
        self.dcnt = {q: 0 for q in self.ring}
        self.engs = ("pe", "act", "dve", "pool", "sp")
        self.ops = {e: [] for e in self.engs}
        self.waited = {e: {} for e in self.engs}
        self.last = {}
        self.dma_last = {}
        self.pending = {e: [] for e in self.engs}
        self.nops = 0

    def _add_dep(self, deps, tok, eng, kind):
        if tok is None:
            return
        if tok[3] == eng:
            if eng == "pe" or kind != "raw":
                return
        deps.append(tok)

    def op(self, eng, fn, reads=(), writes=(), pwrites=(), dma=False):
        deps = []
        for t in reads:
            for tok in t.writers.values():
                self._add_dep(deps, tok, eng, "raw")
        for t in writes:
            for tok in t.writers.values():
                self._add_dep(deps, tok, eng, "waw")
            for tok in t.readers.values():
                self._add_dep(deps, tok, eng, "war")
            for tok in t.war.values():
                self._add_dep(deps, tok, eng, "war")
        for t in pwrites:
            for tok in t.readers.values():
                self._add_dep(deps, tok, eng, "war")
            for tok in t.base.values():
                self._add_dep(deps, tok, eng, "waw")
        xs = []
        for t in list(reads) + list(writes) + list(pwrites):
            if t.excl is not None and t.excl not in xs:
                xs.append(t.excl)
        for x in xs:
            for tok in x.writers.values():
                if tok[3] != eng or eng == "dma":
                    deps.append(tok)
        if self.pending[eng]:
            deps.extend(self.pending[eng])
            self.pending[eng] = []
        waits = []
        if dma:
            q = eng
            ring = self.ring[q]
            i = self.dcnt[q]
            self.dcnt[q] += 1
            sem = ring[i % len(ring)]
            val = 16 * (i // len(ring) + 1)
            name = "d_%s%d" % (q, i % len(ring))
            if val > 16:
                deps.append((name, sem, val - 16, "dma"))
            tok = (name, sem, val, "dma")
            key = "dma:%s:%d" % (q, i)
            inc = (sem, 16)
            self.dma_last[(q, i % len(ring))] = tok
        else:
            self.cnt[eng] += 1
            tok = ("s_" + eng, self.sem[eng], self.cnt[eng], eng)
            key = eng
            inc = (self.sem[eng], 1)
            self.last[eng] = tok
        w = self.waited[eng]
        for d in deps:
            if w.get(d[0], 0) >= d[2]:
                continue
            w[d[0]] = d[2]
            waits.append((d[1], d[2]))
        self.ops[eng].append((waits, fn, inc))
        self.nops += 1
        for t in reads:
            t.readers[key] = tok
        for t in writes:
            t.writers = {key: tok}
            t.base = {key: tok}
            t.readers = {}
            t.war = {}
        for t in pwrites:
            t.writers[key] = tok
        for x in xs:
            x.writers = {key: tok}
        return tok

    def barrier(self):
        toks = list(self.last.values()) + list(self.dma_last.values())
        for e in self.engs:
            self.pending[e] = list(toks)

    def wait_all(self, eng):
        toks = list(self.last.values()) + list(self.dma_last.values())
        w = self.waited[eng]
        waits = []
        for d in toks:
            if d[3] == eng and eng != "sp":
                continue
            if w.get(d[0], 0) >= d[2]:
                continue
            w[d[0]] = d[2]
            waits.append((d[1], d[2]))
        self.ops[eng].append((waits, None, None))

    def emit(self):
        nc = self.nc
        ops = self.ops
        with nc.Block() as block:
            def run(E, lst):
                for waits, fn, inc in lst:
                    for sem, val in waits:
                        E.wait_ge(sem, val)
                    if fn is not None:
                        ins = fn(E)
                        ins.then_inc(inc[0], inc[1])

            @block.sync
            def _(E):
                run(E, ops["sp"])

            @block.tensor
            def _(E):
                run(E, ops["pe"])

            @block.scalar
            def _(E):
                run(E, ops["act"])

            @block.vector
            def _(E):
                run(E, ops["dve"])

            @block.gpsimd
            def _(E):
                run(E, ops["pool"])
        self.ops = {e: [] for e in self.engs}


class Ctx:
    pass


class View:
    def __init__(self, buf, fn):
        self.tl = buf.tl
        self._fn = fn

    def __getitem__(self, idx):
        return self._fn()


def alloc_sb(nc, es, name, shape, dt):
    return Buf(es.enter_context(nc.sbuf_tensor("sb_" + name, list(shape), dt)), name)


def alloc_ps(nc, es, name, shape, dt):
    isz = 2 if dt == BF16 else 4
    nel = 1
    for d_ in shape[1:]:
        nel *= d_
    nb = (nel * isz + 2047) // 2048
    raw = es.enter_context(nc.psum_tensor("ps_" + name, [128, nb * 512], F32))
    ap = raw[:, :]
    if dt == BF16:
        ap = ap.bitcast(BF16)
    ap = ap[:, 0:nel]
    if len(shape) == 3:
        ap = ap.rearrange("p (a b) -> p a b", b=shape[2])
    b_ = Buf(ap, name)
    b_.tl.excl = Tl("x_" + name)
    return b_


def make_consts():
    j = np.arange(128)[:, None]
    i = np.arange(128)[None, :]
    c = np.zeros((128, NCONST, 128), np.float32)
    c[:, 0] = (j == i)
    c[:, 1] = (j <= i)
    c[:, 2] = (j >= i)
    c[:, 3] = np.where(i >= j, 0.0, NEG)
    c[:, 4] = np.where(i <= j, 0.0, NEG)
    c[:, 5] = (i > j)
    c[:, 6] = (i < j)
    c[:, 7] = 1.0
    blk = lambda s_: (j // s_ == i // s_).astype(np.float32)
    c[:, 8] = blk(8)
    c[:, 9] = blk(16) - blk(8)
    c[:, 10] = blk(32) - blk(16)
    c[:, 11] = blk(64) - blk(32)
    c[:, 12] = blk(128) - blk(64)
    return c


def phase_ada(nc, S, G, I):
    with ExitStack() as es:
        cT = alloc_sb(nc, es, "cT", [128, 8, 3], F32)
        sT = alloc_sb(nc, es, "sT", [128, 8, 3], F32)
        srep = alloc_sb(nc, es, "srep", [128, 2, 8, 128], F32)
        badaT = alloc_sb(nc, es, "badaT", [128, 48], F32)
        n1T = alloc_sb(nc, es, "n1T", [128, 8], F32)
        n2T = alloc_sb(nc, es, "n2T", [128, 8], F32)
        bbc = alloc_sb(nc, es, "bbc", [128, 2, 1024], F32)
        wblk = [alloc_sb(nc, es, "wblk%d" % i, [128, 8, 512], F32) for i in range(2)]
        modT = alloc_sb(nc, es, "modT", [128, 48, 3], F32)
        psm = alloc_ps(nc, es, "psm", [128, 48, 3], F32)
        psg = [alloc_ps(nc, es, "psg%d" % i, [128, 512], F32) for i in range(2)]

        S.op("sp", lambda E: E.dma_start(out=cT[:], in_=I["cT"][:]), writes=[cT.tl], dma=True)
        S.op("sp", lambda E: E.dma_start(out=badaT[:], in_=I["badaT"][:]), writes=[badaT.tl], dma=True)
        S.op("sp", lambda E: E.dma_start(out=n1T[:], in_=I["n1T"][:]), writes=[n1T.tl], dma=True)
        S.op("sp", lambda E: E.dma_start(out=n2T[:], in_=I["n2T"][:]), writes=[n2T.tl], dma=True)
        for gi, c0 in enumerate((2048, 5120)):
            S.op("sp", lambda E, gi=gi, c0=c0: E.dma_start(
                out=bbc[:, gi, :], in_=I["b_ada"][0, c0:c0 + 1024].partition_broadcast(128)),
                pwrites=[bbc.tl], dma=True)
        S.op("act", lambda E: E.activation(out=sT[:], in_=cT[:], func=AF.Silu),
             reads=[cT.tl], writes=[sT.tl])
        for v in range(2):
            S.op("dve", lambda E, v=v: E.tensor_copy(
                out=srep[:, v], in_=sT[:, :, v:v + 1].broadcast_to([128, 8, 128])),
                reads=[sT.tl], pwrites=[srep.tl])
        nblk = 0
        for blk in range(12):
            wb = wblk[nblk % 2]
            nblk += 1
            S.op("sp", lambda E, wb=wb, blk=blk: E.dma_start(
                out=wb[:], in_=I["w_ada"][:, blk * 512:(blk + 1) * 512].rearrange("(k p) c -> p k c", p=128)),
                writes=[wb.tl], dma=True)
            if blk in (4, 5, 10, 11):
                gi = 0 if blk < 6 else 1
                half = blk % 2 if blk < 6 else (blk - 10)
                for v in range(2):
                    pg = psg[v]
                    for kc in range(8):
                        S.op("pe", lambda E, pg=pg, v=v, kc=kc, wb=wb: E.matmul(
                            pg[:], lhsT=srep[:, v, kc, :], rhs=wb[:, kc, :], start=(kc == 0), stop=(kc == 7)),
                            reads=[srep.tl, wb.tl], writes=[pg.tl])
                    S.op("dve", lambda E, pg=pg, v=v, gi=gi, half=half: E.tensor_tensor(
                        out=G.gbc[:, v, gi, half * 512:(half + 1) * 512], in0=pg[:],
                        in1=bbc[:, gi, half * 512:(half + 1) * 512], op=ALU.add),
                        reads=[pg.tl, bbc.tl], pwrites=[G.gbc.tl])
            else:
                for sub in range(4):
                    m = blk * 4 + sub
                    for kc in range(8):
                        S.op("pe", lambda E, m=m, kc=kc, wb=wb, sub=sub: E.matmul(
                            psm[:, m, :], lhsT=wb[:, kc, sub * 128:(sub + 1) * 128], rhs=sT[:, kc, :],
                            start=(kc == 0), stop=(kc == 7)),
                            reads=[sT.tl, wb.tl], pwrites=[psm.tl])
        S.op("dve", lambda E: E.tensor_tensor(
            out=modT[:], in0=psm[:], in1=badaT[:].unsqueeze(2).broadcast_to([128, 48, 3]), op=ALU.add),
            reads=[psm.tl, badaT.tl], writes=[modT.tl])
        for (dst_s, dst_b, nT, m_sh, m_sc) in ((G.scaleA, G.biasA, n1T, 0, 8), (G.scaleB, G.biasB, n2T, 24, 32)):
            S.op("dve", lambda E, dst_s=dst_s, nT=nT, m_sc=m_sc: E.scalar_tensor_tensor(
                out=dst_s[:], in0=modT[:, m_sc:m_sc + 8, :], scalar=1.0,
                in1=nT[:].unsqueeze(2).broadcast_to([128, 8, 3]), op0=ALU.add, op1=ALU.mult),
                reads=[modT.tl, nT.tl], writes=[dst_s.tl])
            S.op("dve", lambda E, dst_b=dst_b, m_sh=m_sh: E.tensor_copy(out=dst_b[:], in_=modT[:, m_sh:m_sh + 8, :]),
                 reads=[modT.tl], writes=[dst_b.tl])
        S.emit()
    S.barrier()


def phase_norm(nc, S, G, I, hT):
    with ExitStack() as es:
        xt = [alloc_sb(nc, es, "xt%d" % i, [128, 1024], F32) for i in range(2)]
        xn = [alloc_sb(nc, es, "xn%d" % i, [128, 1024], F32) for i in range(2)]
        junk = alloc_sb(nc, es, "junk", [128, 1024], F32)
        st = [alloc_sb(nc, es, "st%d" % i, [128, 4], F32) for i in range(2)]
        tmp = [alloc_sb(nc, es, "tmpm%d" % i, [128, 8, 128], F32) for i in range(2)]
        pst = [alloc_ps(nc, es, "pst%d" % i, [128, 8, 128], F32) for i in range(2)]
        n = 0
        for b in range(2):
            for tt in range(NTT):
                X, XN, ST, PT, TM = xt[n % 2], xn[n % 2], st[n % 2], pst[n % 2], tmp[n % 2]
                n += 1
                if tt < 2:
                    src = I["ctx"][b, tt * 128:(tt + 1) * 128, :]
                    v = 2
                else:
                    src = I["x"][b, (tt - 2) * 128:(tt - 1) * 128, :]
                    v = b
                S.op("sp", lambda E, X=X, src=src: E.dma_start(out=X[:], in_=src), writes=[X.tl], dma=True)
                S.op("act", lambda E, X=X, ST=ST: E.activation(out=junk[:], in_=X[:], func=AF.Square,
                                                               accum_out=ST[:, 0:1]),
                     reads=[X.tl], writes=[junk.tl], pwrites=[ST.tl])
                S.op("act", lambda E, ST=ST: E.activation(out=ST[:, 1:2], in_=ST[:, 0:1], func=AF.Sqrt,
                                                          bias=G.epsc[:, 0:1], scale=1.0 / D),
                     reads=[ST.tl, G.epsc.tl], pwrites=[ST.tl])
                S.op("dve", lambda E, ST=ST: E.reciprocal(out=ST[:, 2:3], in_=ST[:, 1:2]),
                     reads=[ST.tl], pwrites=[ST.tl])
                S.op("dve", lambda E, X=X, XN=XN, ST=ST: E.tensor_scalar(
                    out=XN[:], in0=X[:], scalar1=ST[:, 2:3], scalar2=None, op0=ALU.mult),
                    reads=[X.tl, ST.tl], writes=[XN.tl])
                for kc in range(8):
                    S.op("pe", lambda E, XN=XN, PT=PT, kc=kc: E.transpose(
                        out=PT[:, kc, :], in_=XN[:, kc * 128:(kc + 1) * 128], identity=G.ident[:]),
                        reads=[XN.tl, G.ident.tl], writes=[PT.tl])
                S.op("dve", lambda E, PT=PT, TM=TM, v=v: E.tensor_tensor(
                    out=TM[:], in0=PT[:], in1=G.scaleA[:, :, v:v + 1].broadcast_to([128, 8, 128]), op=ALU.mult),
                    reads=[PT.tl, G.scaleA.tl], writes=[TM.tl])
                S.op("pool", lambda E, TM=TM, v=v, b=b, tt=tt: E.tensor_tensor(
                    out=hT[:, b, :, tt * 128:(tt + 1) * 128], in0=TM[:],
                    in1=G.biasA[:, :, v:v + 1].broadcast_to([128, 8, 128]), op=ALU.add),
                    reads=[TM.tl, G.biasA.tl], writes=[hT.sub((b, tt))])
        S.emit()
    S.barrier()


def load_w_bf16(nc, S, dst, src, c0, c1):
    nk = src.shape[0] // 128
    for kc in range(nk):
        S.op("pool", lambda E, kc=kc: E.dma_start(out=dst[:, kc, :], in_=src[kc * 128:(kc + 1) * 128, c0:c1],
                                                  max_dma_last_dim=4096),
             writes=[dst.sub(kc)], dma=True)


def phase_c1(nc, S, G, I, W, hT):
    with ExitStack() as es:
        winA = alloc_sb(nc, es, "winA", [128, 8, 1536], BF16)
        convT = alloc_sb(nc, es, "convT", [128, 12, 5], F32)
        eps128 = alloc_sb(nc, es, "eps128", [128, 1], F32)
        pxT = [alloc_sb(nc, es, "pxT%d" % i, [128, 2312], F32) for i in range(2)]
        acc = alloc_sb(nc, es, "acc", [128, NT], F32)
        sq = alloc_sb(nc, es, "sq", [128, NT], BF16)
        rsq = alloc_sb(nc, es, "rsq", [128, NT], F32)
        obf = [alloc_sb(nc, es, "obf%d" % i, [128, NT], BF16) for i in range(2)]
        psx = [alloc_ps(nc, es, "psx%d" % i, [128, 512], F32) for i in range(2)]
        pss = [alloc_ps(nc, es, "pss%d" % i, [128, 512], F32) for i in range(2)]
        load_w_bf16(nc, S, winA, I["w_in"], 0, 1536)
        S.op("sp", lambda E: E.dma_start(out=convT[:], in_=I["convT"][:]), writes=[convT.tl], dma=True)
        S.op("dve", lambda E: E.memset(eps128[:], 128.0 * EPS), writes=[eps128.tl])
        for p in pxT:
            S.op("dve", lambda E, p=p: E.memset(p[:], 0.0), writes=[p.tl])
        groups = [(0, 256, 2)] + [(256 + 512 * g, 512, 262 + 512 * g) for g in range(4)]
        segs = [(0, 256, 0), (256, 2048, 260)]
        nx = 0
        no = 0
        for b in range(2):
            for m in range(12):
                P_ = pxT[(b * 12 + m) % 2]
                for (t0, n, pc) in groups:
                    ps = psx[nx % 2]
                    nx += 1
                    for kc in range(8):
                        S.op("pe", lambda E, ps=ps, kc=kc, m=m, b=b, t0=t0, n=n: E.matmul(
                            ps[:, 0:n], lhsT=winA[:, kc, m * 128:(m + 1) * 128], rhs=hT[:, b, kc, t0:t0 + n],
                            start=(kc == 0), stop=(kc == 7)),
                            reads=[winA.sub(kc)] + [hT.sub((b, tt)) for tt in range(t0 // 128, (t0 + n) // 128)],
                            writes=[ps.tl])
                    S.op("act", lambda E, ps=ps, P_=P_, pc=pc, n=n: E.copy(out=P_[:, pc:pc + n], in_=ps[:, 0:n]),
                         reads=[ps.tl], pwrites=[P_.tl])
                for (t0, n, pb) in segs:
                    S.op("dve", lambda E, P_=P_, t0=t0, n=n, pb=pb, m=m: E.tensor_scalar(
                        out=acc[:, t0:t0 + n], in0=P_[:, pb:pb + n], scalar1=convT[:, m, 0:1], scalar2=None,
                        op0=ALU.mult), reads=[P_.tl, convT.tl], pwrites=[acc.tl])
                    for t in range(1, 5):
                        S.op("dve", lambda E, P_=P_, t0=t0, n=n, pb=pb, m=m, t=t: E.scalar_tensor_tensor(
                            out=acc[:, t0:t0 + n], in0=P_[:, pb + t:pb + t + n], scalar=convT[:, m, t:t + 1],
                            in1=acc[:, t0:t0 + n], op0=ALU.mult, op1=ALU.add),
                            reads=[P_.tl, convT.tl, acc.tl], pwrites=[acc.tl])
                O = obf[no % 2]
                no += 1
                if m >= 8:
                    S.op("act", lambda E, O=O: E.activation(out=O[:], in_=acc[:], func=AF.Silu),
                         reads=[acc.tl], writes=[O.tl])
                else:
                    S.op("act", lambda E: E.activation(out=acc[:], in_=acc[:], func=AF.Silu),
                         reads=[acc.tl], writes=[acc.tl])
                    S.op("pool", lambda E: E.tensor_tensor(out=sq[:], in0=acc[:], in1=acc[:], op=ALU.mult),
                         reads=[acc.tl], writes=[sq.tl])
                    for (t0, n, pc) in groups:
                        ps = pss[nx % 2]
                        nx += 1
                        S.op("pe", lambda E, ps=ps, t0=t0, n=n: E.matmul(
                            ps[:, 0:n], lhsT=G.onesb[:], rhs=sq[:, t0:t0 + n], start=True, stop=True),
                            reads=[G.onesb.tl, sq.tl], writes=[ps.tl])
                        sc = 128.0 if m < 4 else 1.0
                        ep = eps128 if m < 4 else G.epsc
                        S.op("act", lambda E, ps=ps, t0=t0, n=n, sc=sc, ep=ep: E.activation(
                            out=rsq[:, t0:t0 + n], in_=ps[:, 0:n], func=AF.Sqrt, bias=ep[:, 0:1], scale=sc),
                            reads=[ps.tl, ep.tl], pwrites=[rsq.tl])
                    S.op("dve", lambda E: E.reciprocal(out=rsq[:], in_=rsq[:]), reads=[rsq.tl], writes=[rsq.tl])
                    S.op("dve", lambda E, O=O: E.tensor_tensor(out=O[:], in0=acc[:], in1=rsq[:], op=ALU.mult),
                         reads=[acc.tl, rsq.tl], writes=[O.tl])
                S.op("sp", lambda E, O=O, b=b, m=m: E.dma_start(out=W.qkvT[b, m * 128:(m + 1) * 128, :], in_=O[:]),
                     reads=[O.tl], pwrites=[W.t_qkvT[b]], dma=True)
        S.emit()
    S.barrier()


def phase_c2(nc, S, G, I, W, hT):
    with ExitStack() as es:
        winB = alloc_sb(nc, es, "winB", [128, 8, 2064], BF16)
        dtb = alloc_sb(nc, es, "dtb", [128, 8], F32)
        negA = alloc_sb(nc, es, "negA", [128, 8], F32)
        wq = alloc_sb(nc, es, "wq", [128, 64], F32)
        wk = alloc_sb(nc, es, "wk", [128, 64], F32)
        onec = alloc_sb(nc, es, "onec", [128, 1], F32)
        zt = [alloc_sb(nc, es, "zt%d" % i, [128, 512], F32) for i in range(2)]
        vt = [alloc_sb(nc, es, "vt%d" % i, [128, 512], BF16) for i in range(2)]
        gbt = [alloc_sb(nc, es, "gbt%d" % i, [128, 16], F32) for i in range(2)]
        g8 = [alloc_sb(nc, es, "g8%d" % i, [128, 3, 8], F32) for i in range(2)]
        sqf = alloc_sb(nc, es, "sqf", [128, 512], F32)
        ss8 = [alloc_sb(nc, es, "ss8%d" % i, [128, 2, 8], F32) for i in range(2)]
        t1 = alloc_sb(nc, es, "t1", [128, 512], F32)
        qn = [alloc_sb(nc, es, "qn%d" % i, [128, 512], BF16) for i in range(2)]
        qtT = [alloc_sb(nc, es, "qtT%d" % i, [128, 4, 128], BF16) for i in range(2)]
        psb = [alloc_ps(nc, es, "psb%d" % i, [128, 512], F32) for i in range(3)]
        psab = alloc_ps(nc, es, "psab", [128, 16], F32)
        pstr = [alloc_ps(nc, es, "pstr%d" % i, [128, 4, 128], BF16) for i in range(2)]
        load_w_bf16(nc, S, winB, I["w_in"], 1536, 3600)
        S.op("sp", lambda E: E.dma_start(out=dtb[:], in_=I["dtb"][0, :].partition_broadcast(128)),
             writes=[dtb.tl], dma=True)
        S.op("sp", lambda E: E.dma_start(out=negA[:], in_=I["alog"][0, :].partition_broadcast(128)),
             writes=[negA.tl], dma=True)
        S.op("sp", lambda E: E.dma_start(out=wq[:], in_=I["qnw"][0, :].partition_broadcast(128)),
             writes=[wq.tl], dma=True)
        S.op("sp", lambda E: E.dma_start(out=wk[:], in_=I["knw"][0, :].partition_broadcast(128)),
             writes=[wk.tl], dma=True)
        S.op("act", lambda E: E.activation(out=negA[:], in_=negA[:], func=AF.Exp), reads=[negA.tl], writes=[negA.tl])
        S.op("dve", lambda E: E.tensor_scalar(out=negA[:], in0=negA[:], scalar1=-1.0, scalar2=None, op0=ALU.mult),
             reads=[negA.tl], writes=[negA.tl])
        S.op("dve", lambda E: E.tensor_scalar(out=wq[:], in0=wq[:], scalar1=0.125, scalar2=None, op0=ALU.mult),
             reads=[wq.tl], writes=[wq.tl])
        S.op("dve", lambda E: E.memset(onec[:], 1.0), writes=[onec.tl])
        npb = 0
        n2 = 0
        for b in range(2):
            for tt in range(NTT):
                lat = tt >= 2
                hrd = [hT.sub((b, tt))]
                tk = slice(tt * 128, (tt + 1) * 128)

                def proj(c0, n, ps, b=b, tk=tk, hrd=hrd):
                    for kc in range(8):
                        S.op("pe", lambda E, kc=kc, c0=c0, n=n, ps=ps, b=b, tk=tk: E.matmul(
                            ps[:, 0:n], lhsT=hT[:, b, kc, tk], rhs=winB[:, kc, c0:c0 + n],
                            start=(kc == 0), stop=(kc == 7)),
                            reads=hrd + [winB.sub(kc)], writes=[ps.tl])

                k2 = n2 % 2
                n2 += 1
                if lat:
                    ps = psb[npb % 3]
                    npb += 1
                    proj(0, 512, ps)
                    Z = zt[k2]
                    S.op("act", lambda E, ps=ps, Z=Z: E.activation(out=Z[:], in_=ps[:], func=AF.Silu),
                         reads=[ps.tl], writes=[Z.tl])
                    S.op("sp", lambda E, Z=Z, b=b, tt=tt: E.dma_start(
                        out=W.zs[b, (tt - 2) * 128:(tt - 1) * 128, :], in_=Z[:]),
                        reads=[Z.tl], pwrites=[W.t_zs[b]], dma=True)
                proj(512, 16, psab)
                GB, G8 = gbt[k2], g8[k2]
                S.op("dve", lambda E, G8=G8: E.tensor_tensor(out=G8[:, 0, :], in0=psab[:, 0:8], in1=dtb[:], op=ALU.add),
                     reads=[psab.tl, dtb.tl], pwrites=[G8.tl])
                S.op("act", lambda E, G8=G8: E.activation(out=G8[:, 1, :], in_=G8[:, 0, :], func=AF.Exp),
                     reads=[G8.tl], pwrites=[G8.tl])
                S.op("act", lambda E, G8=G8: E.activation(out=G8[:, 2, :], in_=psab[:, 8:16], func=AF.Exp, scale=-1.0),
                     reads=[psab.tl], pwrites=[G8.tl])
                S.op("act", lambda E, G8=G8: E.activation(out=G8[:, 1, :], in_=G8[:, 1, :], func=AF.Ln,
                                                          bias=onec[:, 0:1], scale=1.0),
                     reads=[G8.tl, onec.tl], pwrites=[G8.tl])
                S.op("dve", lambda E, G8=G8, GB=GB: E.tensor_tensor(out=GB[:, 0:8], in0=G8[:, 1, :], in1=negA[:], op=ALU.mult),
                     reads=[G8.tl, negA.tl], pwrites=[GB.tl])
                S.op("dve", lambda E, G8=G8: E.tensor_scalar(out=G8[:, 2, :], in0=G8[:, 2, :], scalar1=1.0, scalar2=None,
                                                             op0=ALU.add), reads=[G8.tl], pwrites=[G8.tl])
                S.op("dve", lambda E, G8=G8, GB=GB: E.reciprocal(out=GB[:, 8:16], in_=G8[:, 2, :]),
                     reads=[G8.tl], pwrites=[GB.tl])
                S.op("sp", lambda E, GB=GB, b=b, tk=tk: E.dma_start(out=W.gb[b, tk, :], in_=GB[:]),
                     reads=[GB.tl], pwrites=[W.t_gb[b]], dma=True)
                ps = psb[npb % 3]
                npb += 1
                proj(1552, 512, ps)
                V = vt[k2]
                S.op("act", lambda E, ps=ps, V=V: E.copy(out=V[:], in_=ps[:]), reads=[ps.tl], writes=[V.tl])
                S.op("sp", lambda E, V=V, b=b, tk=tk: E.dma_start(out=W.naV[b, tk, :], in_=V[:]),
                     reads=[V.tl], pwrites=[W.t_naV[b]], dma=True)
                for which in ((1,) if not lat else (1, 0)):
                    ps = psb[npb % 3]
                    npb += 1
                    proj(528 + 512 * which, 512, ps)
                    SS = ss8[which]
                    QN, QT, PT = qn[which], qtT[which], pstr[which]
                    wn = wq if which == 0 else wk
                    S.op("act", lambda E, ps=ps: E.activation(out=sqf[:], in_=ps[:], func=AF.Square),
                         reads=[ps.tl], writes=[sqf.tl])
                    S.op("dve", lambda E, SS=SS: E.tensor_reduce(
                        out=SS[:, 0, :], in_=sqf[:].rearrange("p (h d) -> p h d", d=64), axis=AX.X, op=ALU.add),
                        reads=[sqf.tl], pwrites=[SS.tl])
                    S.op("act", lambda E, SS=SS: E.activation(out=SS[:, 1, :], in_=SS[:, 0, :], func=AF.Sqrt,
                                                              bias=G.epsc[:, 0:1], scale=1.0 / 64),
                         reads=[SS.tl, G.epsc.tl], pwrites=[SS.tl])
                    S.op("dve", lambda E, SS=SS: E.reciprocal(out=SS[:, 1, :], in_=SS[:, 1, :]),
                         reads=[SS.tl], pwrites=[SS.tl])
                    S.op("dve", lambda E, SS=SS, ps=ps: E.tensor_tensor(
                        out=t1[:].rearrange("p (h d) -> p h d", d=64), in0=ps[:].rearrange("p (h d) -> p h d", d=64),
                        in1=SS[:, 1, :].unsqueeze(2).broadcast_to([128, 8, 64]), op=ALU.mult),
                        reads=[SS.tl, ps.tl], writes=[t1.tl])
                    S.op("dve", lambda E, QN=QN, wn=wn: E.tensor_tensor(
                        out=QN[:].rearrange("p (h d) -> p h d", d=64), in0=t1[:].rearrange("p (h d) -> p h d", d=64),
                        in1=wn[:].unsqueeze(1).broadcast_to([128, 8, 64]), op=ALU.mult),
                        reads=[t1.tl, wn.tl], writes=[QN.tl])
                    for j in range(4):
                        S.op("pe", lambda E, QN=QN, PT=PT, j=j: E.transpose(
                            out=PT[:, j, :], in_=QN[:, j * 128:(j + 1) * 128], identity=G.identb[:]),
                            reads=[QN.tl, G.identb.tl], writes=[PT.tl])
                    S.op("act", lambda E, QT=QT, PT=PT: E.copy(out=QT[:], in_=PT[:]), reads=[PT.tl], writes=[QT.tl])
                    if which == 0:
                        dst = W.naQT[b, :, (tt - 2) * 128:(tt - 1) * 128]
                        tlw = W.t_naQT[b]
                    else:
                        dst = W.naKT[b, :, tk]
                        tlw = W.t_naKT[b]
                    S.op("sp", lambda E, QT=QT, dst=dst: E.dma_start(
                        out=dst.rearrange("(j p) t -> p j t", p=128), in_=QT[:]),
                        reads=[QT.tl], pwrites=[tlw], dma=True)
        S.emit()
    S.barrier()


def precast_ffn_in(nc, S, I, W):
    for j in range(22):
        for s_ in range(2):
            c0 = s_ * DFF + j * 128
            S.op("pool", lambda E, j=j, s_=s_, c0=c0: E.dma_start(
                out=W.wf1s[j, :, :, s_, :], in_=I["w_ffn_in"][:, c0:c0 + 128].rearrange("(k p) c -> p k c", p=128)),
                pwrites=[W.t_wf1s], dma=True)


def phase_f(nc, S, G, I, W, out):
    with ExitStack() as es:
        woutb = alloc_sb(nc, es, "woutb", [128, 8, 1024], BF16)
        wf2 = alloc_sb(nc, es, "wf2", [128, 22, 1024], BF16)
        wst = [alloc_sb(nc, es, "wst%d" % i, [128, 8, 2, 128], BF16) for i in range(4)]
        x1 = alloc_sb(nc, es, "x1", [128, 4, 1024], F32)
        h2T = alloc_sb(nc, es, "h2T", [128, 8, 512], BF16)
        act = alloc_sb(nc, es, "actb", [128, 22, 512], BF16)
        xt = [alloc_sb(nc, es, "fxt%d" % i, [128, 1024], F32) for i in range(2)]
        mixt = [alloc_sb(nc, es, "mixt%d" % i, [128, 1024], BF16) for i in range(2)]
        mixT = alloc_sb(nc, es, "mixT", [128, 8, 128], BF16)
        xn = alloc_sb(nc, es, "fxn", [128, 1024], F32)
        tmpm = alloc_sb(nc, es, "ftmpm", [128, 8, 128], F32)
        sg = alloc_sb(nc, es, "sg", [128, 512], F32)
        tmp = [alloc_sb(nc, es, "ftmp%d" % i, [128, 512], F32) for i in range(2)]
        ot = [alloc_sb(nc, es, "ot%d" % i, [128, 1024], F32) for i in range(2)]
        junk = alloc_sb(nc, es, "fjunk", [128, 1024], F32)
        st = [alloc_sb(nc, es, "fst%d" % i, [128, 4], F32) for i in range(2)]
        psA = [alloc_ps(nc, es, "psA%d" % i, [128, 512], F32) for i in range(2)]
        psT = alloc_ps(nc, es, "psT", [128, 8, 128], F32)
        psTb = alloc_ps(nc, es, "psTb", [128, 8, 128], BF16)
        psGU = [alloc_ps(nc, es, "psGU%d" % i, [128, 512], F32) for i in range(3)]
        load_w_bf16(nc, S, woutb, I["w_out"], 0, 1024)
        load_w_bf16(nc, S, wf2, I["w_ffn_out"], 0, 1024)
        nA = 0
        nG = 0
        nW = 0
        nt_ = 0
        for g in range(8):
            b = g // 4
            for i in range(4):
                tok0 = (g % 4) * 512 + i * 128
                X, M_, ST = xt[nt_ % 2], mixt[nt_ % 2], st[nt_ % 2]
                nt_ += 1
                S.op("sp", lambda E, X=X, b=b, tok0=tok0: E.dma_start(out=X[:], in_=I["x"][b, tok0:tok0 + 128, :]),
                     writes=[X.tl], dma=True)
                S.op("sp", lambda E, M_=M_, b=b, tok0=tok0: E.dma_start(out=M_[:], in_=W.mix[b, tok0:tok0 + 128, :]),
                     reads=[W.t_mix[b]], writes=[M_.tl], dma=True)
                for kc in range(8):
                    S.op("pe", lambda E, M_=M_, kc=kc: E.transpose(
                        out=psTb[:, kc, :], in_=M_[:, kc * 128:(kc + 1) * 128], identity=G.identb[:]),
                        reads=[M_.tl, G.identb.tl], writes=[psTb.tl])
                S.op("act", lambda E: E.copy(out=mixT[:], in_=psTb[:]), reads=[psTb.tl], writes=[mixT.tl])
                for half in range(2):
                    ps = psA[nA % 2]
                    T_ = tmp[nA % 2]
                    nA += 1
                    hs = slice(half * 512, (half + 1) * 512)
                    for kc in range(8):
                        S.op("pe", lambda E, ps=ps, kc=kc, hs=hs: E.matmul(
                            ps[:], lhsT=mixT[:, kc, :], rhs=woutb[:, kc, hs], start=(kc == 0), stop=(kc == 7)),
                            reads=[mixT.tl, woutb.sub(kc)], writes=[ps.tl])
                    S.op("dve", lambda E, ps=ps, T_=T_, b=b, hs=hs: E.tensor_tensor(
                        out=T_[:], in0=ps[:], in1=G.gbc[:, b, 0, hs], op=ALU.mult),
                        reads=[ps.tl, G.gbc.tl], writes=[T_.tl])
                    S.op("pool", lambda E, T_=T_, X=X, i=i, hs=hs: E.tensor_tensor(
                        out=x1[:, i, hs], in0=T_[:], in1=X[:, hs], op=ALU.add),
                        reads=[T_.tl, X.tl], pwrites=[x1.sub(i)])
                S.op("act", lambda E, ST=ST, i=i: E.activation(out=junk[:], in_=x1[:, i, :], func=AF.Square,
                                                               accum_out=ST[:, 0:1]),
                     reads=[x1.sub(i)], writes=[junk.tl], pwrites=[ST.tl])
                S.op("act", lambda E, ST=ST: E.activation(out=ST[:, 1:2], in_=ST[:, 0:1], func=AF.Sqrt,
                                                          bias=G.epsc[:, 0:1], scale=1.0 / D),
                     reads=[ST.tl, G.epsc.tl], pwrites=[ST.tl])
                S.op("dve", lambda E, ST=ST: E.reciprocal(out=ST[:, 2:3], in_=ST[:, 1:2]), reads=[ST.tl], pwrites=[ST.tl])
                S.op("dve", lambda E, ST=ST, i=i: E.tensor_scalar(
                    out=xn[:], in0=x1[:, i, :], scalar1=ST[:, 2:3], scalar2=None, op0=ALU.mult),
                    reads=[x1.sub(i), ST.tl], writes=[xn.tl])
                for kc in range(8):
                    S.op("pe", lambda E, kc=kc: E.transpose(
                        out=psT[:, kc, :], in_=xn[:, kc * 128:(kc + 1) * 128], identity=G.ident[:]),
                        reads=[xn.tl, G.ident.tl], writes=[psT.tl])
                S.op("dve", lambda E, b=b: E.tensor_tensor(
                    out=tmpm[:], in0=psT[:], in1=G.scaleB[:, :, b:b + 1].broadcast_to([128, 8, 128]), op=ALU.mult),
                    reads=[psT.tl, G.scaleB.tl], writes=[tmpm.tl])
                S.op("pool", lambda E, b=b, i=i: E.tensor_tensor(
                    out=h2T[:, :, i * 128:(i + 1) * 128], in0=tmpm[:],
                    in1=G.biasB[:, :, b:b + 1].broadcast_to([128, 8, 128]), op=ALU.add),
                    reads=[tmpm.tl, G.biasB.tl], pwrites=[h2T.tl])
            for j in range(22):
                WS = wst[nW % 4]
                nW += 1
                S.op("sp", lambda E, WS=WS, j=j: E.dma_start(out=WS[:], in_=W.wf1s[j]),
                     reads=[W.t_wf1s], writes=[WS.tl], dma=True)
                pg = psGU[nG % 3]
                pu = psGU[(nG + 1) % 3]
                nG += 2
                for s_, ps in ((0, pg), (1, pu)):
                    for kc in range(8):
                        S.op("pe", lambda E, ps=ps, kc=kc, s_=s_, WS=WS: E.matmul(
                            ps[:], lhsT=WS[:, kc, s_, :], rhs=h2T[:, kc, :], start=(kc == 0), stop=(kc == 7)),
                            reads=[WS.tl, h2T.tl], writes=[ps.tl])
                S.op("act", lambda E, pg=pg: E.activation(out=sg[:], in_=pg[:], func=AF.Silu),
                     reads=[pg.tl], writes=[sg.tl])
                S.op("dve", lambda E, pu=pu, j=j: E.tensor_tensor(out=act[:, j, :], in0=sg[:], in1=pu[:], op=ALU.mult),
                     reads=[sg.tl, pu.tl], pwrites=[act.tl])
            for i in range(4):
                tok0 = (g % 4) * 512 + i * 128
                O = ot[i % 2]
                for half in range(2):
                    ps = psA[nA % 2]
                    T_ = tmp[nA % 2]
                    nA += 1
                    hs = slice(half * 512, (half + 1) * 512)
                    for j in range(22):
                        S.op("pe", lambda E, ps=ps, j=j, i=i, hs=hs: E.matmul(
                            ps[:], lhsT=act[:, j, i * 128:(i + 1) * 128], rhs=wf2[:, j, hs],
                            start=(j == 0), stop=(j == 21)),
                            reads=[act.tl, wf2.sub(j)], writes=[ps.tl])
                    S.op("dve", lambda E, ps=ps, T_=T_, b=b, hs=hs: E.tensor_tensor(
                        out=T_[:], in0=ps[:], in1=G.gbc[:, b, 1, hs], op=ALU.mult),
                        reads=[ps.tl, G.gbc.tl], writes=[T_.tl])
                    S.op("pool", lambda E, T_=T_, O=O, i=i, hs=hs: E.tensor_tensor(
                        out=O[:, hs], in0=T_[:], in1=x1[:, i, hs], op=ALU.add),
                        reads=[T_.tl, x1.sub(i)], pwrites=[O.tl])
                S.op("sp", lambda E, O=O, b=b, tok0=tok0: E.dma_start(out=out[b, tok0:tok0 + 128, :], in_=O[:]),
                     reads=[O.tl], dma=True)
        S.emit()
    S.barrier()


def phase_na(nc, S, G, I, W):
    with ExitStack() as es:
        KT = [alloc_sb(nc, es, "naK%d" % i, [128, NT], BF16) for i in range(2)]
        QT = [alloc_sb(nc, es, "naQ%d" % i, [128, SEQ], BF16) for i in range(2)]
        VA = [alloc_sb(nc, es, "naVa%d" % i, [128, NTT, 65], BF16) for i in range(2)]
        BT = [alloc_sb(nc, es, "naB%d" % i, [128, 21, 128], BF16) for i in range(2)]
        PT = [alloc_sb(nc, es, "naP%d" % i, [128, 8, 128], BF16) for i in range(2)]
        rc = [alloc_sb(nc, es, "narc%d" % i, [128, 1], F32) for i in range(2)]
        naout = alloc_sb(nc, es, "naout", [128, 16, 512], BF16)
        psS = [alloc_ps(nc, es, "psS%d" % i, [128, 8, 128], F32) for i in range(2)]
        psO = [alloc_ps(nc, es, "psO%d" % i, [128, 65], F32) for i in range(2)]
        for i in range(2):
            S.op("dve", lambda E, i=i: E.memset(KT[i][:], 0.0), writes=[KT[i].tl])
            S.op("dve", lambda E, i=i: E.memset(QT[i][:], 0.0), writes=[QT[i].tl])
            S.op("dve", lambda E, i=i: E.memset(VA[i][:], 1.0), writes=[VA[i].tl])
        n = 0
        nq = 0
        for b in range(2):
            for h in range(8):
                K_, Q_, V_, B_ = KT[n % 2], QT[n % 2], VA[n % 2], BT[n % 2]
                n += 1
                hs = slice(h * 64, (h + 1) * 64)
                S.op("sp", lambda E, K_=K_, b=b, hs=hs: E.dma_start(out=K_[0:64, :], in_=W.naKT[b, hs, :]),
                     reads=[W.t_naKT[b]], pwrites=[K_.tl], dma=True)
                S.op("sp", lambda E, Q_=Q_, b=b, hs=hs: E.dma_start(out=Q_[0:64, :], in_=W.naQT[b, hs, :]),
                     reads=[W.t_naQT[b]], pwrites=[Q_.tl], dma=True)
                S.op("sp", lambda E, V_=V_, b=b, hs=hs: E.dma_start(
                    out=V_[:, :, 0:64], in_=W.naV[b, :, hs].rearrange("(c p) d -> p c d", p=128)),
                    reads=[W.t_naV[b]], pwrites=[V_.tl], dma=True)
                S.op("pool", lambda E, B_=B_, h=h: E.dma_start(out=B_[:], in_=I["rpbt"][h], max_dma_last_dim=4096),
                     writes=[B_.tl], dma=True)
                for t in range(16):
                    slots = [(2 + c, ti) for (c, ti) in na_tile_chunks(t)] + [(0, None), (1, None)]
                    PS, PO, P_, RC = psS[nq % 2], psO[nq % 2], PT[nq % 2], rc[nq % 2]
                    nq += 1
                    qs = slice(t * 128, (t + 1) * 128)
                    for sl, (kc_, ti) in enumerate(slots):
                        ks = slice(kc_ * 128, (kc_ + 1) * 128)
                        S.op("pe", lambda E, PS=PS, sl=sl, K_=K_, Q_=Q_, ks=ks, qs=qs, ti=ti: E.matmul(
                            PS[:, sl, :], lhsT=K_[:, ks], rhs=Q_[:, qs], start=True, stop=(ti is None)),
                            reads=[K_.tl, Q_.tl], writes=[PS.tl])
                        if ti is not None:
                            S.op("pe", lambda E, PS=PS, sl=sl, B_=B_, ti=ti: E.matmul(
                                PS[:, sl, :], lhsT=G.identb[:], rhs=B_[:, ti, :], start=False, stop=True),
                                reads=[G.identb.tl, B_.tl], writes=[PS.tl])
                    ns = len(slots)
                    S.op("act", lambda E, PS=PS, P_=P_: E.activation(out=P_[:, 0:4, :], in_=PS[:, 0:4, :], func=AF.Exp),
                         reads=[PS.tl], pwrites=[P_.tl])
                    S.op("act", lambda E, PS=PS, P_=P_, ns=ns: E.activation(out=P_[:, 4:ns, :], in_=PS[:, 4:ns, :],
                                                                           func=AF.Exp),
                         reads=[PS.tl], pwrites=[P_.tl])
                    for sl, (kc_, ti) in enumerate(slots):
                        S.op("pe", lambda E, PO=PO, P_=P_, V_=V_, sl=sl, kc_=kc_, ns=ns: E.matmul(
                            PO[:], lhsT=P_[:, sl, :], rhs=V_[:, kc_, :], start=(sl == 0), stop=(sl == ns - 1)),
                            reads=[P_.tl, V_.tl], writes=[PO.tl])
                    S.op("dve", lambda E, PO=PO, RC=RC: E.reciprocal(out=RC[:], in_=PO[:, 64:65]),
                         reads=[PO.tl], writes=[RC.tl])
                    S.op("dve", lambda E, PO=PO, RC=RC, t=t, hs=hs: E.tensor_scalar(
                        out=naout[:, t, hs], in0=PO[:, 0:64], scalar1=RC[:, 0:1], scalar2=None, op0=ALU.mult),
                        reads=[PO.tl, RC.tl], pwrites=[naout.tl])
            S.op("sp", lambda E, b=b: E.dma_start(
                out=W.mix[b, :, 512:1024].rearrange("(t p) c -> p t c", p=128), in_=naout[:]),
                reads=[naout.tl], pwrites=[W.t_mix[b]], dma=True)
        S.emit()
    S.barrier()


def phase_dn(nc, S, G, I, W):
    C = G.consts
    with ExitStack() as es:
        qT = alloc_sb(nc, es, "dqT", [128, 4, NT], BF16)
        kT = alloc_sb(nc, es, "dkT", [128, 4, NT], BF16)
        vT = alloc_sb(nc, es, "dvT", [128, 4, NT], BF16)
        gb = alloc_sb(nc, es, "dgb", [128, NTT, 16], F32)
        onw = alloc_sb(nc, es, "donw", [128, 128], F32)
        oacc = alloc_sb(nc, es, "oacc", [128, 16, 512], F32)
        osum = [alloc_sb(nc, es, "osum%d" % i, [128, 512], F32) for i in range(2)]
        Gm = [alloc_sb(nc, es, "Gm%d" % i, [128, 4, 128], F32) for i in range(2)]
        tmp4 = alloc_sb(nc, es, "tmp4", [128, 4, 128], F32)
        gamT = [alloc_sb(nc, es, "gamT%d" % i, [128, 4], F32) for i in range(2)]
        KVtok = [alloc_sb(nc, es, "KVtok%d" % i, [128, 8, 128], BF16) for i in range(2)]
        zst = [alloc_sb(nc, es, "zst%d" % i, [128, 512], F32) for i in range(2)]
        sqo = alloc_sb(nc, es, "sqo", [128, 512], F32)
        y1 = alloc_sb(nc, es, "y1", [128, 512], F32)
        ss4 = alloc_sb(nc, es, "ss4", [128, 2, 4], F32)
        ybf = [alloc_sb(nc, es, "ybf%d" % i, [128, 512], BF16) for i in range(2)]
        sets = []
        for dh in range(8):
            T = Ctx()
            for nm, shp, dt in (("Dm", [128, 128], F32), ("ETi", [128, 128], F32), ("Zf", [128, 128], F32),
                                ("eg", [128, 128], F32), ("Sf", [128, 128], F32), ("col", [128, 4], F32),
                                ("Zf2", [128, 128], F32), ("Sf2", [128, 128], F32),
                                ("NT", [128, 128], BF16), ("N", [128, 128], BF16), ("Pd", [128, 128], BF16),
                                ("Nd", [128, 128], BF16), ("Y1", [128, 128], BF16), ("Y1T", [128, 128], BF16),
                                ("Y2T", [128, 128], BF16), ("Zb", [128, 128], BF16), ("DbT", [128, 128], BF16),
                                ("OT", [128, 128], BF16), ("Xb", [128, 128], BF16),
                                ("qkT", [128, 128], BF16), ("KgT", [128, 128], BF16),
                                ("QgT", [128, 128], BF16), ("Kd", [128, 128], BF16), ("R", [128, 128], BF16),
                                ("vn", [128, 128], BF16), ("Sb", [128, 128], BF16)):
                setattr(T, nm, alloc_sb(nc, es, "s%d_%s" % (dh, nm), shp, dt))
            sets.append(T)
        ps_gam = [alloc_ps(nc, es, "ps_gam%d" % i, [128, 4, 128], F32) for i in range(2)]
        ps_tr = alloc_ps(nc, es, "ps_tr", [128, 8, 128], BF16)
        ps_kkq = alloc_ps(nc, es, "ps_kkq", [128, 4, 128], F32)
        ps_inv = [alloc_ps(nc, es, "ps_inv%d" % i, [128, 4, 128], F32) for i in range(2)]
        ps_rec = [alloc_ps(nc, es, "ps_rec%d" % i, [128, 4, 128], F32) for i in range(2)]
        S.op("sp", lambda E: E.dma_start(out=onw[:], in_=I["onw"][0, :].partition_broadcast(128)),
             writes=[onw.tl], dma=True)
        cnt = {"u": 0, "kkq": 0, "inv": 0, "rec": 0, "ep": 0}
        for b in range(2):
            for (dst, r0) in ((qT, 0), (kT, 512), (vT, 1024)):
                for hh in HH_ORDER:
                    S.op("sp", lambda E, dst=dst, r0=r0, hh=hh, b=b: E.dma_start(
                        out=dst[:, hh, :], in_=W.qkvT[b, r0 + hh * 128:r0 + (hh + 1) * 128, :]),
                        reads=[W.t_qkvT[b]], writes=[dst.sub(hh)], dma=True)
            S.op("sp", lambda E, b=b: E.dma_start(out=gb[:], in_=W.gb[b].rearrange("(c p) k -> p c k", p=128)),
                 reads=[W.t_gb[b]], writes=[gb.tl], dma=True)
            for dh in range(8):
                T = sets[dh]
                S.op("pool", lambda E, T=T: E.memset(T.Sf[:], 0.0), writes=[T.Sf.tl])
                S.op("pool", lambda E, T=T: E.memset(T.Sb[:], 0.0), writes=[T.Sb.tl])
            border = [1, 0] + list(range(17, 1, -1))
            for step in range(DN_STEPS):
                for d in range(2):
                    c = step if d == 0 else border[step]
                    lat = c >= 2
                    cs = slice(c * 128, (c + 1) * 128)
                    u = cnt["u"] % 2
                    cnt["u"] += 1
                    GM, PG, GT, KV = Gm[u], ps_gam[u], gamT[u], KVtok[u]
                    tri, mneg, s01 = (1, 3, 5) if d == 0 else (2, 4, 6)
                    lastcol = 127 if d == 0 else 0
                    S.op("dve", lambda E, GM=GM, tri=tri, c=c, d=d: E.tensor_tensor(
                        out=GM[:], in0=C[:, tri:tri + 1, :].broadcast_to([128, 4, 128]),
                        in1=gb[:, c, d * 4:(d + 1) * 4].unsqueeze(2).broadcast_to([128, 4, 128]), op=ALU.mult),
                        reads=[C.tl, gb.tl], writes=[GM.tl])
                    S.op("pe", lambda E, GM=GM, PG=PG: E.matmul(
                        PG[:].rearrange("p a b -> p (a b)"), lhsT=C[:, 7, :], rhs=GM[:].rearrange("p a b -> p (a b)"),
                        start=True, stop=True), reads=[C.tl, GM.tl], writes=[PG.tl])
                    S.op("dve", lambda E, PG=PG: E.tensor_tensor(
                        out=tmp4[:], in0=PG[:], in1=C[:, 0:1, :].broadcast_to([128, 4, 128]), op=ALU.mult),
                        reads=[PG.tl, C.tl], writes=[tmp4.tl])
                    S.op("dve", lambda E, GT=GT: E.tensor_reduce(out=GT[:], in_=tmp4[:], axis=AX.X, op=ALU.add),
                         reads=[tmp4.tl], writes=[GT.tl])
                    for hh in HH_ORDER:
                        S.op("pe", lambda E, hh=hh, cs=cs: E.transpose(
                            out=ps_tr[:, hh, :], in_=kT[:, hh, cs], identity=G.identb[:]),
                            reads=[kT.sub(hh), G.identb.tl], writes=[ps_tr.tl])
                        S.op("pe", lambda E, hh=hh, cs=cs: E.transpose(
                            out=ps_tr[:, 4 + hh, :], in_=vT[:, hh, cs], identity=G.identb[:]),
                            reads=[vT.sub(hh), G.identb.tl], writes=[ps_tr.tl])
                    S.op("act", lambda E, KV=KV: E.copy(out=KV[:], in_=ps_tr[:]), reads=[ps_tr.tl], writes=[KV.tl])
                    if DN_STAGE < 1:
                        continue
                    for hh in HH_ORDER:
                        dh = d * 4 + hh
                        T = sets[dh]
                        k0 = (cnt["kkq"] % 2) * 2
                        cnt["kkq"] += 1
                        S.op("pe", lambda E, hh=hh, cs=cs, k0=k0: E.matmul(
                            ps_kkq[:, k0, :], lhsT=kT[:, hh, cs], rhs=kT[:, hh, cs], start=True, stop=True),
                            reads=[kT.sub(hh)], writes=[ps_kkq.sub(k0)])
                        if lat:
                            S.op("pe", lambda E, hh=hh, cs=cs, k0=k0: E.matmul(
                                ps_kkq[:, k0 + 1, :], lhsT=kT[:, hh, cs], rhs=qT[:, hh, cs], start=True, stop=True),
                                reads=[kT.sub(hh), qT.sub(hh)], writes=[ps_kkq.sub(k0 + 1)])
                        S.op("dve", lambda E, T=T, PG=PG, GT=GT, hh=hh, mneg=mneg: E.scalar_tensor_tensor(
                            out=T.Dm[:], in0=PG[:, hh, :], scalar=GT[:, hh:hh + 1], in1=C[:, mneg, :],
                            op0=ALU.subtract, op1=ALU.add), reads=[PG.tl, GT.tl, C.tl], writes=[T.Dm.tl])
                        S.op("act", lambda E, T=T: E.activation(out=T.ETi[:], in_=T.Dm[:], func=AF.Exp),
                             reads=[T.Dm.tl], writes=[T.ETi.tl])
                        S.op("pool", lambda E, T=T, s01=s01: E.tensor_tensor(
                            out=T.Dm[:], in0=T.ETi[:], in1=C[:, s01, :], op=ALU.mult),
                            reads=[T.ETi.tl, C.tl], writes=[T.Dm.tl])
                        S.op("dve", lambda E, T=T, k0=k0, c=c, dh=dh: E.scalar_tensor_tensor(
                            out=T.NT[:], in0=ps_kkq[:, k0, :], scalar=gb[:, c, 8 + dh:9 + dh], in1=T.Dm[:],
                            op0=ALU.mult, op1=ALU.mult), reads=[ps_kkq.sub(k0), gb.tl, T.Dm.tl], writes=[T.NT.tl])
                        if lat:
                            S.op("act", lambda E, T=T, k0=k0: E.copy(out=T.eg[:], in_=ps_kkq[:, k0 + 1, :]),
                                 reads=[ps_kkq.sub(k0 + 1)], writes=[T.eg.tl])
                            S.op("dve", lambda E, T=T: E.tensor_tensor(
                                out=T.qkT[:], in0=T.eg[:], in1=T.ETi[:], op=ALU.mult),
                                reads=[T.eg.tl, T.ETi.tl], writes=[T.qkT.tl])
                    if DN_STAGE < 2:
                        continue
                    def islot():
                        k = cnt["inv"] % 8
                        cnt["inv"] += 1
                        return ps_inv[k // 4], k % 4

                    def mm_to(T, lhsT, lhs_tl, rhs, rhs_tl, dst, dst_tl, eng="act"):
                        PB, k = islot()
                        S.op("pe", lambda E, PB=PB, k=k, lhsT=lhsT, rhs=rhs: E.matmul(
                            PB[:, k, :], lhsT=lhsT, rhs=rhs, start=True, stop=True),
                            reads=[lhs_tl, rhs_tl], writes=[PB.sub(k)])
                        S.op("act", lambda E, PB=PB, k=k, dst=dst: E.copy(out=dst, in_=PB[:, k, :]),
                             reads=[PB.sub(k)], writes=[dst_tl])

                    hs_ = [sets[d * 4 + hh] for hh in HH_ORDER]
                    for T in hs_:
                        mm_to(T, T.NT[:], T.NT.tl, G.identb[:], G.identb.tl, T.N[:], T.N.tl)
                        S.op("pool", lambda E, T=T: E.tensor_tensor(out=T.Pd[:], in0=T.NT[:], in1=C[:, 8, :], op=ALU.mult),
                             reads=[T.NT.tl, C.tl], writes=[T.Pd.tl])
                    for T in hs_:
                        S.op("pool", lambda E, T=T: E.tensor_tensor(out=T.Nd[:], in0=T.N[:], in1=C[:, 8, :], op=ALU.mult),
                             reads=[T.N.tl, C.tl], writes=[T.Nd.tl])
                        S.op("dve", lambda E, T=T: E.scalar_tensor_tensor(
                            out=T.Zf[:], in0=T.Pd[:], scalar=-1.0, in1=C[:, 0, :], op0=ALU.mult, op1=ALU.add),
                            reads=[T.Pd.tl, C.tl], writes=[T.Zf.tl])
                        S.op("pool", lambda E, T=T: E.tensor_copy(out=T.Zb[:], in_=T.Zf[:]),
                             reads=[T.Zf.tl], writes=[T.Zb.tl])
                    if DN_STAGE < 2.3:
                        continue
                    for T in hs_:
                        mm_to(T, T.Nd[:], T.Nd.tl, T.Pd[:], T.Pd.tl, T.Y1[:], T.Y1.tl)
                        mm_to(T, T.Pd[:], T.Pd.tl, T.Nd[:], T.Nd.tl, T.Y1T[:], T.Y1T.tl)
                    for T in hs_:
                        mm_to(T, T.Y1T[:], T.Y1T.tl, T.Zb[:], T.Zb.tl, T.eg[:], T.eg.tl)
                        mm_to(T, T.Y1[:], T.Y1.tl, T.Y1T[:], T.Y1T.tl, T.Y2T[:], T.Y2T.tl)
                        S.op("dve", lambda E, T=T: E.tensor_tensor(out=T.Zf2[:], in0=T.eg[:], in1=T.Zf[:], op=ALU.add),
                             reads=[T.eg.tl, T.Zf.tl], writes=[T.Zf2.tl])
                        S.op("pool", lambda E, T=T: E.tensor_copy(out=T.Zb[:], in_=T.Zf2[:]),
                             reads=[T.Zf2.tl], writes=[T.Zb.tl])
                    for T in hs_:
                        mm_to(T, T.Y2T[:], T.Y2T.tl, T.Zb[:], T.Zb.tl, T.eg[:], T.eg.tl)
                        S.op("dve", lambda E, T=T: E.tensor_tensor(out=T.Zf[:], in0=T.eg[:], in1=T.Zf2[:], op=ALU.add),
                             reads=[T.eg.tl, T.Zf2.tl], writes=[T.Zf.tl])
                        S.op("pool", lambda E, T=T: E.tensor_copy(out=T.Zb[:], in_=T.Zf[:]),
                             reads=[T.Zf.tl], writes=[T.Zb.tl])
                    if DN_STAGE < 2.6:
                        continue
                    for T in hs_:
                        mm_to(T, T.Zb[:], T.Zb.tl, G.identb[:], G.identb.tl, T.DbT[:], T.DbT.tl)
                    for lvl in range(4):
                        Dc, Dn = (("Zf", "Zf2") if lvl % 2 == 0 else ("Zf2", "Zf"))
                        for T in hs_:
                            S.op("pool", lambda E, T=T, lvl=lvl: E.tensor_tensor(
                                out=T.OT[:], in0=T.N[:], in1=C[:, 9 + lvl, :], op=ALU.mult),
                                reads=[T.N.tl, C.tl], writes=[T.OT.tl])
                            mm_to(T, T.OT[:], T.OT.tl, T.Zb[:], T.Zb.tl, T.Xb[:], T.Xb.tl)
                        for T in hs_:
                            Dcur, Dnew = getattr(T, Dc), getattr(T, Dn)
                            mm_to(T, T.DbT[:], T.DbT.tl, T.Xb[:], T.Xb.tl, T.eg[:], T.eg.tl)
                            S.op("dve", lambda E, T=T, Dcur=Dcur, Dnew=Dnew: E.tensor_tensor(
                                out=Dnew[:], in0=Dcur[:], in1=T.eg[:], op=ALU.subtract),
                                reads=[Dcur.tl, T.eg.tl], writes=[Dnew.tl])
                            S.op("pool", lambda E, T=T, Dnew=Dnew: E.tensor_copy(out=T.Zb[:], in_=Dnew[:]),
                                 reads=[Dnew.tl], writes=[T.Zb.tl])
                        if lvl < 3:
                            for T in hs_:
                                mm_to(T, T.Zb[:], T.Zb.tl, G.identb[:], G.identb.tl, T.DbT[:], T.DbT.tl)
                    if DN_STAGE < 3:
                        continue
                    first = (d == 0 and c <= 9) or (d == 1 and c >= 10)
                    for hh in HH_ORDER:
                        dh = d * 4 + hh
                        T = sets[dh]
                        PR = ps_rec[cnt["rec"] % 2]
                        cnt["rec"] += 1
                        S.op("act", lambda E, T=T, PG=PG, hh=hh: E.activation(out=T.eg[:], in_=PG[:, hh, :], func=AF.Exp),
                             reads=[PG.tl], writes=[T.eg.tl])
                        S.op("dve", lambda E, T=T, hh=hh, cs=cs: E.tensor_tensor(
                            out=T.KgT[:], in0=kT[:, hh, cs], in1=T.eg[:], op=ALU.mult),
                            reads=[kT.sub(hh), T.eg.tl], writes=[T.KgT.tl])
                        if lat:
                            S.op("pool", lambda E, T=T, hh=hh, cs=cs: E.tensor_tensor(
                                out=T.QgT[:], in0=qT[:, hh, cs], in1=T.eg[:], op=ALU.mult),
                                reads=[qT.sub(hh), T.eg.tl], writes=[T.QgT.tl])
                        S.op("dve", lambda E, T=T, PG=PG, GT=GT, hh=hh, lastcol=lastcol: E.tensor_scalar(
                            out=T.col[:, 0:1], in0=PG[:, hh, lastcol:lastcol + 1], scalar1=GT[:, hh:hh + 1], scalar2=None,
                            op0=ALU.subtract),
                            reads=[PG.tl, GT.tl], pwrites=[T.col.tl])
                        S.op("act", lambda E, T=T: E.activation(out=T.col[:, 1:2], in_=T.col[:, 0:1], func=AF.Exp),
                             reads=[T.col.tl], pwrites=[T.col.tl])
                        S.op("act", lambda E, T=T, PG=PG, hh=hh, lastcol=lastcol: E.activation(
                            out=T.col[:, 2:3], in_=PG[:, hh, lastcol:lastcol + 1], func=AF.Exp),
                            reads=[PG.tl], pwrites=[T.col.tl])
                        S.op("dve", lambda E, T=T, KV=KV, hh=hh: E.tensor_scalar(
                            out=T.Kd[:], in0=KV[:, hh, :], scalar1=T.col[:, 1:2], scalar2=None, op0=ALU.mult),
                            reads=[KV.tl, T.col.tl], writes=[T.Kd.tl])
                        S.op("pe", lambda E, T=T, PR=PR: E.matmul(PR[:, 0, :], lhsT=T.KgT[:], rhs=T.Sb[:],
                                                                  start=True, stop=True),
                             reads=[T.KgT.tl, T.Sb.tl], writes=[PR.sub(0)])
                        S.op("act", lambda E, T=T, PR=PR: E.copy(out=T.ETi[:], in_=PR[:, 0, :]),
                             reads=[PR.sub(0)], writes=[T.ETi.tl])
                        S.op("dve", lambda E, T=T, KV=KV, hh=hh: E.tensor_tensor(
                            out=T.R[:], in0=KV[:, 4 + hh, :], in1=T.ETi[:], op=ALU.subtract),
                            reads=[KV.tl, T.ETi.tl], writes=[T.R.tl])
                        S.op("pe", lambda E, T=T, PR=PR: E.matmul(PR[:, 1, :], lhsT=T.Zb[:], rhs=T.R[:],
                                                                  start=True, stop=True),
                             reads=[T.Zb.tl, T.R.tl], writes=[PR.sub(1)])
                        S.op("dve", lambda E, T=T, PR=PR, c=c, dh=dh: E.tensor_scalar(
                            out=T.vn[:], in0=PR[:, 1, :], scalar1=gb[:, c, 8 + dh:9 + dh], scalar2=None, op0=ALU.mult),
                            reads=[PR.sub(1), gb.tl], writes=[T.vn.tl])
                        if lat:
                            S.op("pe", lambda E, T=T, PR=PR: E.matmul(PR[:, 2, :], lhsT=T.qkT[:], rhs=T.vn[:],
                                                                      start=True, stop=False),
                                 reads=[T.qkT.tl, T.vn.tl], writes=[PR.sub(2)])
                            S.op("pe", lambda E, T=T, PR=PR: E.matmul(PR[:, 2, :], lhsT=T.QgT[:], rhs=T.Sb[:],
                                                                      start=False, stop=True),
                                 reads=[T.QgT.tl, T.Sb.tl], writes=[PR.sub(2)])
                            osl = oacc[:, c - 2, hh * 128:(hh + 1) * 128]
                            otl = oacc.sub((c - 2, hh))
                            if first:
                                S.op("act", lambda E, PR=PR, osl=osl: E.copy(out=osl, in_=PR[:, 2, :]),
                                     reads=[PR.sub(2)], writes=[otl])
                            else:
                                OS = osum[cnt["ep"] % 2]
                                S.op("act", lambda E, T=T, PR=PR: E.copy(out=T.Dm[:], in_=PR[:, 2, :]),
                                     reads=[PR.sub(2)], writes=[T.Dm.tl])
                                S.op("dve", lambda E, T=T, osl=osl, OS=OS, hh=hh: E.tensor_tensor(
                                    out=OS[:, hh * 128:(hh + 1) * 128], in0=T.Dm[:], in1=osl, op=ALU.add),
                                    reads=[T.Dm.tl, otl], writes=[OS.sub(hh)])
                        S.op("pe", lambda E, T=T, PR=PR: E.matmul(PR[:, 3, :], lhsT=T.Kd[:], rhs=T.vn[:],
                                                                  start=True, stop=True),
                             reads=[T.Kd.tl, T.vn.tl], writes=[PR.sub(3)])
                        Ss, Sd = (T.Sf, T.Sf2) if step % 2 == 0 else (T.Sf2, T.Sf)
                        S.op("dve", lambda E, T=T, PR=PR, Ss=Ss, Sd=Sd: E.scalar_tensor_tensor(
                            out=Sd[:], in0=Ss[:], scalar=T.col[:, 2:3], in1=PR[:, 3, :], op0=ALU.mult, op1=ALU.add),
                            reads=[Ss.tl, T.col.tl, PR.sub(3)], writes=[Sd.tl])
                        S.op("act", lambda E, T=T, Sd=Sd: E.copy(out=T.Sb[:], in_=Sd[:]), reads=[Sd.tl], writes=[T.Sb.tl])
                    if lat and not first:
                        e = cnt["ep"] % 2
                        cnt["ep"] += 1
                        Z, Y = zst[e], ybf[e]
                        OS = osum[e]
                        lc = c - 2
                        otls = [OS.sub(hh) for hh in range(4)]
                        S.op("sp", lambda E, Z=Z, b=b, lc=lc: E.dma_start(out=Z[:], in_=W.zs[b, lc * 128:(lc + 1) * 128, :]),
                             reads=[W.t_zs[b]], writes=[Z.tl], dma=True)
                        S.op("act", lambda E, OS=OS: E.activation(out=sqo[:], in_=OS[:], func=AF.Square),
                             reads=otls, writes=[sqo.tl])
                        S.op("dve", lambda E: E.tensor_reduce(
                            out=ss4[:, 0, :], in_=sqo[:].rearrange("p (h d) -> p h d", d=128), axis=AX.X, op=ALU.add),
                            reads=[sqo.tl], pwrites=[ss4.tl])
                        S.op("act", lambda E: E.activation(out=ss4[:, 1, :], in_=ss4[:, 0, :], func=AF.Sqrt,
                                                           bias=G.epsc[:, 0:1], scale=1.0 / 128),
                             reads=[ss4.tl, G.epsc.tl], pwrites=[ss4.tl])
                        S.op("dve", lambda E: E.reciprocal(out=ss4[:, 1, :], in_=ss4[:, 1, :]),
                             reads=[ss4.tl], pwrites=[ss4.tl])
                        S.op("dve", lambda E, OS=OS: E.tensor_tensor(
                            out=y1[:].rearrange("p (h d) -> p h d", d=128),
                            in0=OS[:].rearrange("p (h d) -> p h d", d=128),
                            in1=ss4[:, 1, :].unsqueeze(2).broadcast_to([128, 4, 128]), op=ALU.mult),
                            reads=otls + [ss4.tl], writes=[y1.tl])
                        S.op("pool", lambda E: E.tensor_tensor(
                            out=y1[:].rearrange("p (h d) -> p h d", d=128), in0=y1[:].rearrange("p (h d) -> p h d", d=128),
                            in1=onw[:].unsqueeze(1).broadcast_to([128, 4, 128]), op=ALU.mult),
                            reads=[y1.tl, onw.tl], writes=[y1.tl])
                        S.op("dve", lambda E, Z=Z, Y=Y: E.tensor_tensor(out=Y[:], in0=y1[:], in1=Z[:], op=ALU.mult),
                             reads=[y1.tl, Z.tl], writes=[Y.tl])
                        S.op("sp", lambda E, Y=Y, b=b, lc=lc: E.dma_start(
                            out=W.mix[b, lc * 128:(lc + 1) * 128, 0:512], in_=Y[:]),
                            reads=[Y.tl], pwrites=[W.t_mix[b]], dma=True)
        S.emit()
    S.barrier()


IN_SPECS = [
    ("x", [2, SEQ, D], F32), ("ctx", [2, CTX, D], F32), ("cT", [128, 8, 3], F32),
    ("w_ada", [D, 6 * D], F32), ("b_ada", [1, 6 * D], F32), ("badaT", [128, 48], F32),
    ("n1T", [128, 8], F32), ("n2T", [128, 8], F32), ("w_in", [D, INC], F32),
    ("convT", [128, 12, 5], F32), ("alog", [1, 8], F32), ("dtb", [1, 8], F32),
    ("onw", [1, 128], F32), ("qnw", [1, 64], F32), ("knw", [1, 64], F32),
    ("rpbt", [8, 128, 21, 128], F32), ("w_out", [D, D], F32), ("w_ffn_in", [D, 2 * DFF], F32),
    ("w_ffn_out", [DFF, D], F32), ("consts", [128, NCONST, 128], F32),
]


def build_nc(upto="all", dbg=()):
    nc = bass.Bass("TRN2", target_bir_lowering=False)
    I = {}
    for name, shape, dt in IN_SPECS:
        I[name] = nc.dram_tensor(name, shape, dt, kind="ExternalInput").ap()
    out = nc.dram_tensor("out", [2, SEQ, D], F32, kind="ExternalOutput").ap()

    def scratch(name, shape, dt):
        kind = "ExternalOutput" if name in dbg else "Internal"
        return nc.dram_tensor(name, shape, dt, kind=kind).ap()

    W = Ctx()
    W.qkvT = scratch("qkvT", [2, 1536, NT], BF16)
    W.zs = scratch("zs", [2, SEQ, 512], F32)
    W.gb = scratch("gb", [2, NT, 16], F32)
    W.naQT = scratch("naQT", [2, 512, SEQ], BF16)
    W.naKT = scratch("naKT", [2, 512, NT], BF16)
    W.naV = scratch("naV", [2, NT, 512], BF16)
    if "mix_in" in dbg:
        W.mix = nc.dram_tensor("mix", [2, SEQ, D], BF16, kind="ExternalInput").ap()
    else:
        W.mix = scratch("mix", [2, SEQ, D], BF16)
    W.wf1s = scratch("wf1s", [22, 128, 8, 2, 128], BF16)
    W.t_wf1s = Tl("wf1s")
    W.hTd = scratch("hTd", [2, 128, 8, NT], BF16) if "hTd" in dbg else None
    for nm in ("qkvT", "zs", "gb", "naQT", "naKT", "naV", "mix"):
        setattr(W, "t_" + nm, [Tl("%s%d" % (nm, b)) for b in range(2)])

    with ExitStack() as top:
        S = Sched(nc, top)
        G = Ctx()
        G.consts = alloc_sb(nc, top, "consts", [128, NCONST, 128], F32)
        G.ident = View(G.consts, lambda: G.consts[:, 0, :])
        G.identb = alloc_sb(nc, top, "identb", [128, 128], BF16)
        G.onesb = alloc_sb(nc, top, "onesb", [128, 128], BF16)
        G.epsc = alloc_sb(nc, top, "epsc", [128, 1], F32)
        G.gbc = alloc_sb(nc, top, "gbc", [128, 2, 2, 1024], F32)
        G.scaleA = alloc_sb(nc, top, "scaleA", [128, 8, 3], F32)
        G.biasA = alloc_sb(nc, top, "biasA", [128, 8, 3], F32)
        G.scaleB = alloc_sb(nc, top, "scaleB", [128, 8, 3], F32)
        G.biasB = alloc_sb(nc, top, "biasB", [128, 8, 3], F32)
        S.op("sp", lambda E: E.dma_start(out=G.consts[:], in_=I["consts"][:]), writes=[G.consts.tl], dma=True)
        S.op("dve", lambda E: E.tensor_copy(out=G.identb[:], in_=G.consts[:, 0, :]),
             reads=[G.consts.tl], writes=[G.identb.tl])
        S.op("dve", lambda E: E.tensor_copy(out=G.onesb[:], in_=G.consts[:, 7, :]),
             reads=[G.consts.tl], writes=[G.onesb.tl])
        S.op("dve", lambda E: E.memset(G.epsc[:], EPS), writes=[G.epsc.tl])
        precast_ffn_in(nc, S, I, W)
        phase_ada(nc, S, G, I)
        if upto not in ("ada", "fonly"):
            with ExitStack() as es1:
                hT = alloc_sb(nc, es1, "hT", [128, 2, 8, NT], BF16)
                phase_norm(nc, S, G, I, hT)
                if W.hTd is not None:
                    for b in range(2):
                        S.op("sp", lambda E, b=b: E.dma_start(out=W.hTd[b], in_=hT[:, b]),
                             reads=[hT.sub((b, tt)) for tt in range(NTT)], dma=True)
                if upto not in ("norm",):
                    phase_c1(nc, S, G, I, W, hT)
                if upto not in ("norm", "c1"):
                    phase_c2(nc, S, G, I, W, hT)
        if upto in ("dn", "f", "all"):
            phase_dn(nc, S, G, I, W)
        if upto in ("na", "f", "all"):
            phase_na(nc, S, G, I, W)
        if upto in ("f", "all", "fonly"):
            phase_f(nc, S, G, I, W, out)
        S.wait_all("sp")
        S.emit()
    return nc


def na_tile_chunks(t):
    if 2 <= t <= 13:
        return [(c, c - t + 2) for c in range(t - 2, t + 3)]
    e = {0: 0, 1: 1, 14: 2, 15: 3}[t]
    cb = 0 if t < 2 else 12
    return [(cb + k, 5 + 4 * e + k) for k in range(4)]


def make_rpb_tables(rpb):
    H = rpb.shape[0]
    ext = np.concatenate([rpb.reshape(H, -1), np.full((H, 1), NEG, np.float32)], axis=1)
    masked = 15 * 31
    ka, kc = np.divmod(np.arange(128), 64)
    qr, qc = np.divmod(np.arange(128), 64)
    ws = np.clip(qc - 8, 0, 48)
    colvalid = (kc[:, None] >= ws[None, :]) & (kc[:, None] < ws[None, :] + 16)
    dc = kc[:, None] - qc[None, :] + 15
    idx = np.full((21, 128, 128), masked, np.int64)
    todo = [(5, c, ti) for (c, ti) in na_tile_chunks(5)]
    for t in (0, 1, 14, 15):
        todo += [(t, c, ti) for (c, ti) in na_tile_chunks(t)]
    for (t, c, ti) in todo:
        a = 2 * c + ka[:, None]
        r = 2 * t + qr[None, :]
        r0 = np.clip(r - 4, 0, 24)
        valid = (a >= r0) & (a <= r0 + 7) & colvalid
        ii = (a - r + 7) * 31 + dc
        idx[ti] = np.where(valid, ii, masked)
    tab = ext[:, idx]
    return np.ascontiguousarray(tab.transpose(0, 2, 1, 3))


def make_in_maps(inp, ncores=NCORES):
    f = lambda a: np.ascontiguousarray(a, dtype=np.float32)
    colT = lambda v, n: f(v.reshape(n, 128).T)
    shared = {
        "w_ada": f(inp["w_ada"][0]), "b_ada": f(inp["b_ada"][0][None, :]), "badaT": colT(inp["b_ada"][0], 48),
        "n1T": colT(inp["norm1_w"][0], 8), "n2T": colT(inp["norm2_w"][0], 8), "w_in": f(inp["w_in"][0]),
        "convT": f(inp["dn_conv_w"][0].reshape(5, 12, 128).transpose(2, 1, 0)),
        "alog": f(inp["dn_A_log"][0].reshape(1, 8)), "dtb": f(inp["dn_dt_bias"][0].reshape(1, 8)),
        "onw": f(inp["dn_out_norm_w"][0][None, :]), "qnw": f(inp["na_q_norm_w"][0][None, :]),
        "knw": f(inp["na_k_norm_w"][0][None, :]), "rpbt": make_rpb_tables(f(inp["na_rpb"][0])),
        "w_out": f(inp["w_out"][0]), "w_ffn_in": f(inp["w_ffn_in"][0]), "w_ffn_out": f(inp["w_ffn_out"][0]),
        "consts": make_consts(),
    }
    maps = []
    for i in range(ncores):
        b0 = 2 * i
        vecs = np.stack([inp["c"][b0], inp["c"][b0 + 1], inp["c_ctx"]], axis=1)
        m = dict(shared)
        m["x"] = f(inp["x"][b0:b0 + 2])
        m["ctx"] = f(inp["ctx"][b0:b0 + 2])
        m["cT"] = f(vecs.reshape(8, 128, 3).transpose(1, 0, 2))
        maps.append(m)
    return maps


def kernel(**inputs):
    inp = {k: np.asarray(v) for k, v in inputs.items()}
    nc = build_nc("all")
    maps = make_in_maps(inp, ncores=NCORES)
    res = run_bass_kernel_spmd(nc, maps, core_ids=list(range(NCORES)))
    outs = [np.asarray(r["out"], dtype=np.float32) for r in res.results]
    return np.ascontiguousarray(np.concatenate(outs, axis=0))
```
